# Optimizing a Trainium2 kernel written in Bass

```python
import functools
import jax, jax.numpy as jnp
from jax import lax
import numpy as np

D_MODEL = 1024
BATCH = 2
SEQ = 8192
DEPTH = 4
DEC_BATCH = 128
DEC_SEQ = 8
PAST_LEN = 2048
PAGE_SIZE = 128

N_A = DEPTH // 2
N_B = DEPTH - N_A
D_FF = 2816
CONV_K = 31
HEAD_DIM = 64
N_KV_HEADS = 8
GROUPS = ((128, 1), (512, 4), (2048, 16))
N_GROUPS = len(GROUPS)
HEADS_PER_GROUP = N_KV_HEADS
Q_WIDTH = N_GROUPS * HEADS_PER_GROUP * HEAD_DIM
KV_WIDTH = N_KV_HEADS * HEAD_DIM
MAX_WINDOW = 2048
BAND_BLOCK = 128
ROPE_THETA = 500000.0
ROPE_DIM = HEAD_DIM // 4
EPS = 1e-6
ATTN_SCALE = HEAD_DIM ** -0.5

kernel_name = "yoco_conformer_conv_dilated_swa_decoder_step"

F32 = jnp.float32


def _rms_norm(x, g):
    xf = x.astype(F32)
    y = xf * lax.rsqrt(jnp.mean(xf * xf, axis=-1, keepdims=True) + EPS)
    return (y * g.astype(F32)).astype(x.dtype)


def _layer_norm(x, g, b):
    xf = x.astype(F32)
    mu = jnp.mean(xf, axis=-1, keepdims=True)
    var = jnp.mean(jnp.square(xf - mu), axis=-1, keepdims=True)
    return ((xf - mu) * lax.rsqrt(var + EPS) * g.astype(F32) + b.astype(F32)).astype(x.dtype)


def _swiglu(x, g, w_gate, w_up, w_down):
    h = _rms_norm(x, g)
    return (jax.nn.silu(h @ w_gate) * (h @ w_up)) @ w_down


def _rope(x, pos):
    half = ROPE_DIM // 2
    inv = ROPE_THETA ** (-(jnp.arange(0, ROPE_DIM, 2, dtype=F32) / ROPE_DIM))
    ang = pos.astype(F32)[:, None] * inv[None, :]
    cos = jnp.cos(ang)[:, None, :]
    sin = jnp.sin(ang)[:, None, :]
    xr = x[..., :ROPE_DIM].astype(F32)
    x1, x2 = xr[..., :half], xr[..., half:]
    rot = jnp.concatenate([x1 * cos - x2 * sin, x2 * cos + x1 * sin], axis=-1)
    return jnp.concatenate([rot.astype(x.dtype), x[..., ROPE_DIM:]], axis=-1)


def _conv_module(x, prefix, norm_g, w1, b1, dw, dw_b, ln_g, ln_b, w2, b2):
    h = _rms_norm(x, norm_g)
    a = h @ w1 + b1
    u = a[..., :D_MODEL] * jax.nn.sigmoid(a[..., D_MODEL:])
    ext = jnp.concatenate([prefix.astype(u.dtype), u], axis=1)
    c = lax.conv_general_dilated(ext, dw[:, None, :].astype(ext.dtype), window_strides=(1,),
                                 padding='VALID', dimension_numbers=('NWC', 'WIO', 'NWC'),
                                 feature_group_count=D_MODEL) + dw_b
    c = jax.nn.silu(_layer_norm(c, ln_g, ln_b))
    return c @ w2 + b2, ext[:, -(CONV_K - 1):]


def _shared_kv(h, kv_norm, w_k, w_v, k_norm, pos):
    B, T, _ = h.shape
    hn = _rms_norm(h, kv_norm)
    k = (hn @ w_k).reshape(B, T, N_KV_HEADS, HEAD_DIM)
    k = _rope(_rms_norm(k, k_norm), pos)
    v = (hn @ w_v).reshape(B, T, N_KV_HEADS, HEAD_DIM)
    return k, v


def _queries(h, attn_norm, w_q, q_norm, pos):
    B, T, _ = h.shape
    hn = _rms_norm(h, attn_norm)
    q = (hn @ w_q).reshape(B, T, N_GROUPS * HEADS_PER_GROUP, HEAD_DIM)
    q = _rope(_rms_norm(q, q_norm), pos)
    return q.reshape(B, T, N_GROUPS, HEADS_PER_GROUP, HEAD_DIM)


def _dilated_prompt(q_g, dil, back, k, v):
    B, S, H, HD = q_g.shape
    n = S // dil
    Bd = B * dil

    def fold(t):
        return t.reshape(B, n, dil, H, HD).transpose(0, 2, 1, 3, 4).reshape(Bd, n, H, HD)

    pad = (-n) % BAND_BLOCK
    padw = ((0, 0), (0, pad), (0, 0), (0, 0))
    qf, kf, vf = (jnp.pad(fold(t), padw) for t in (q_g, k, v))
    nb = (n + pad) // BAND_BLOCK
    qb = qf.reshape(Bd, nb, BAND_BLOCK, H, HD)

    def band(t):
        tb = t.reshape(Bd, nb, BAND_BLOCK, H, HD)
        prev = jnp.concatenate([jnp.zeros_like(tb[:, :1]), tb[:, :-1]], axis=1)
        return jnp.concatenate([prev, tb], axis=2)

    kb, vb = band(kf), band(vf)
    s = jnp.einsum('bnqhd,bnkhd->bnhqk', qb, kb, preferred_element_type=F32) * ATTN_SCALE
    qi = jnp.arange(BAND_BLOCK)[:, None]
    kj = jnp.arange(2 * BAND_BLOCK)[None, :]
    dist = qi + BAND_BLOCK - kj
    key_pos = jnp.arange(nb)[:, None, None] * BAND_BLOCK - BAND_BLOCK + kj[None]
    mask = (dist >= 0)[None] & (dist <= back)[None] & (key_pos >= 0)
    s = jnp.where(mask[:, None], s, -jnp.inf)
    lse = jax.nn.logsumexp(s, axis=-1)
    p = jnp.exp(s - lse[..., None])
    o = jnp.einsum('bnhqk,bnkhd->bnqhd', p.astype(vb.dtype), vb, preferred_element_type=F32)
    o = o.reshape(Bd, nb * BAND_BLOCK, H, HD)[:, :n]
    lse = lse.transpose(0, 1, 3, 2).reshape(Bd, nb * BAND_BLOCK, H)[:, :n]
    o = o.reshape(B, dil, n, H, HD).transpose(0, 2, 1, 3, 4).reshape(B, S, H, HD)
    lse = lse.reshape(B, dil, n, H).transpose(0, 2, 1, 3).reshape(B, S, H)
    return o, lse


def _dilated_sample(q_g, dil, back, k_all, v_all):
    T = q_g.shape[1]
    L = k_all.shape[1] - T
    rows = L + jnp.arange(T)[:, None] - dil * jnp.arange(back + 1)[None, :]
    valid = rows >= 0
    rows_c = jnp.maximum(rows, 0)
    kg = k_all[:, rows_c]
    vg = v_all[:, rows_c]
    s = jnp.einsum('bthd,btkhd->bhtk', q_g, kg, preferred_element_type=F32) * ATTN_SCALE
    s = jnp.where(valid[None, None], s, -jnp.inf)
    lse = jax.nn.logsumexp(s, axis=-1)
    p = jnp.exp(s - lse[..., None])
    o = jnp.einsum('bhtk,btkhd->bthd', p.astype(vg.dtype), vg, preferred_element_type=F32)
    return o, lse.transpose(0, 2, 1)


def _dilated_mixture(q, attend):
    outs, lses = [], []
    for g, (win, dil) in enumerate(GROUPS):
        o, l = attend(q[:, :, g], dil, win // dil)
        outs.append(o)
        lses.append(l)
    alpha = jax.nn.softmax(jnp.stack(lses), axis=0)
    return jnp.sum(alpha[..., None] * jnp.stack(outs), axis=0)


def _trunk(x, pos, conv_prefix, kv_past,
           ffn_norm, ffn_w_gate, ffn_w_up, ffn_w_down,
           conv_norm, conv_w1, conv_b1, conv_dw, conv_dw_b, conv_ln_g, conv_ln_b, conv_w2, conv_b2,
           kv_norm, w_k, w_v, k_norm, attn_norm, w_q, q_norm, w_o):
    B, T, _ = x.shape
    conv_states = []
    attend = None
    k_buf = v_buf = None
    for layer in range(DEPTH):
        x = x + 0.5 * _swiglu(x, ffn_norm[layer, 0], ffn_w_gate[layer, 0], ffn_w_up[layer, 0], ffn_w_down[layer, 0])
        if layer < N_A:
            o, st = _conv_module(x, conv_prefix[layer], conv_norm[layer], conv_w1[layer], conv_b1[layer],
                                 conv_dw[layer], conv_dw_b[layer], conv_ln_g[layer], conv_ln_b[layer],
                                 conv_w2[layer], conv_b2[layer])
            conv_states.append(st)
        else:
            i = layer - N_A
            q = _queries(x, attn_norm[i], w_q[i], q_norm[i], pos)
            comb = _dilated_mixture(q, attend)
            o = comb.astype(x.dtype).reshape(B, T, KV_WIDTH) @ w_o[i]
        x = x + o
        x = x + 0.5 * _swiglu(x, ffn_norm[layer, 1], ffn_w_gate[layer, 1], ffn_w_up[layer, 1], ffn_w_down[layer, 1])
        if layer == N_A - 1:
            k_new, v_new = _shared_kv(x, kv_norm, w_k, w_v, k_norm, pos)
            if kv_past is None:
                attend = functools.partial(_dilated_prompt, k=k_new, v=v_new)
                keep = min(MAX_WINDOW, T)
                k_buf, v_buf = k_new[:, T - keep:], v_new[:, T - keep:]
            else:
                cache_k, cache_v = kv_past
                k_all = jnp.concatenate([cache_k.astype(k_new.dtype), k_new], axis=1)
                v_all = jnp.concatenate([cache_v.astype(v_new.dtype), v_new], axis=1)
                attend = functools.partial(_dilated_sample, k_all=k_all, v_all=v_all)
                keep = cache_k.shape[1]
                k_buf, v_buf = k_all[:, -keep:], v_all[:, -keep:]
    return x, jnp.stack(conv_states), k_buf, v_buf


def setup_inputs(seed: int = 0) -> dict:
    key = jax.random.key(seed)
    ks = iter(jax.random.split(key, 40))
    nrm = lambda shape, scale: jax.random.normal(next(ks), shape, F32) * scale
    gain = lambda shape: 1.0 + nrm(shape, 0.02)
    L = min(MAX_WINDOW, PAST_LEN)
    D = D_MODEL
    return {
        "x_prompt": nrm((BATCH, SEQ, D), 1.0),
        "x_sample": nrm((DEC_BATCH, DEC_SEQ, D), 1.0),
        "state_conv": nrm((N_A, DEC_BATCH, CONV_K - 1, D), 0.5),
        "cache_k": nrm((DEC_BATCH, L, N_KV_HEADS, HEAD_DIM), 1.0),
        "cache_v": nrm((DEC_BATCH, L, N_KV_HEADS, HEAD_DIM), 1.0),
        "ffn_norm": gain((DEPTH, 2, D)),
        "ffn_w_gate": nrm((DEPTH, 2, D, D_FF), D ** -0.5),
        "ffn_w_up": nrm((DEPTH, 2, D, D_FF), D ** -0.5),
        "ffn_w_down": nrm((DEPTH, 2, D_FF, D), D_FF ** -0.5),
        "conv_norm": gain((N_A, D)),
        "conv_w1": nrm((N_A, D, 2 * D), D ** -0.5),
        "conv_b1": nrm((N_A, 2 * D), 0.02),
        "conv_dw": nrm((N_A, CONV_K, D), CONV_K ** -0.5),
        "conv_dw_b": nrm((N_A, D), 0.02),
        "conv_ln_g": gain((N_A, D)),
        "conv_ln_b": nrm((N_A, D), 0.02),
        "conv_w2": nrm((N_A, D, D), D ** -0.5),
        "conv_b2": nrm((N_A, D), 0.02),
        "kv_norm": gain((D,)),
        "w_k": nrm((D, KV_WIDTH), D ** -0.5),
        "w_v": nrm((D, KV_WIDTH), D ** -0.5),
        "k_norm": gain((HEAD_DIM,)),
        "attn_norm": gain((N_B, D)),
        "w_q": nrm((N_B, D, Q_WIDTH), D ** -0.5),
        "q_norm": gain((N_B, HEAD_DIM)),
        "w_o": nrm((N_B, KV_WIDTH, D), KV_WIDTH ** -0.5),
    }


def reference(x_prompt, x_sample, state_conv, cache_k, cache_v,
              ffn_norm, ffn_w_gate, ffn_w_up, ffn_w_down,
              conv_norm, conv_w1, conv_b1, conv_dw, conv_dw_b, conv_ln_g, conv_ln_b, conv_w2, conv_b2,
              kv_norm, w_k, w_v, k_norm, attn_norm, w_q, q_norm, w_o):
    weights = (ffn_norm, ffn_w_gate, ffn_w_up, ffn_w_down,
               conv_norm, conv_w1, conv_b1, conv_dw, conv_dw_b, conv_ln_g, conv_ln_b, conv_w2, conv_b2,
               kv_norm, w_k, w_v, k_norm, attn_norm, w_q, q_norm, w_o)
    pos_p = jnp.arange(x_prompt.shape[1])
    zero_prefix = jnp.zeros((N_A, x_prompt.shape[0], CONV_K - 1, D_MODEL), x_prompt.dtype)
    y_prompt, conv_p, k_p, v_p = _trunk(x_prompt, pos_p, zero_prefix, None, *weights)
    pos_s = PAST_LEN + jnp.arange(x_sample.shape[1])
    y_sample, conv_s, k_s, v_s = _trunk(x_sample, pos_s, state_conv, (cache_k, cache_v), *weights)
    return (y_prompt, y_sample, conv_p, conv_s, k_p, v_p, k_s, v_s)
```

```python
import numpy as np
import ml_dtypes
from contextlib import ExitStack
import concourse.bass as bass
import concourse.mybir as mybir
from concourse.bass_utils import run_bass_kernel_spmd

F32 = mybir.dt.float32
BF16 = mybir.dt.bfloat16
AF = mybir.ActivationFunctionType
ALU = mybir.AluOpType
AX = mybir.AxisListType

D = 1024; DFF = 2816; NT = 2304; NOWN = 2048; EPS = 1e-6
EPOCH = 4000
import os
STAGE = int(os.environ.get("KSTAGE", "9"))

C_FFN = 0
C_CN = 64
C_B1 = 80
C_DW = 112
C_DWB = 608
C_LNG = 624
C_LNB = 640
C_B2 = 656
C_KVN = 672
C_AN = 680
C_FLAG = 696
C_SEL = 697
NCOL = 705


RSET = {"uT", "us", "cT", "R", "qst", "qsq", "ktokk", "vtok", "kb16", "vb16", "rt0", "rt1", "rt2", "rt3", "kts", "QT",
        "comb", "vt", "E", "accn", "accd", "rden", "Qbd", "skt", "svt", "skT", "Es", "sacc", "sden", "sred0", "sred1", "knew"}


class Prog:
    def __init__(self):
        self.ops = []
        self.last_w = {}
        self.readers = {}
        self.dma_cum = {}

    def op(self, eng, fn, r=(), w=(), dsem=None, ndma=1, inc=16, R=0):
        i = len(self.ops)
        deps = set()
        r = list(r)
        if R or any(((k[0] if isinstance(k, tuple) else k) in RSET) for k in list(r) + list(w)):
            r.append("Rrole")
        for k in r:
            if k in self.last_w:
                deps.add(self.last_w[k])
        for k in w:
            if k in self.last_w:
                deps.add(self.last_w[k])
            deps.update(self.readers.get(k, ()))
        for k in r:
            self.readers.setdefault(k, []).append(i)
        for k in w:
            self.last_w[k] = i
            self.readers[k] = []
        cum = None
        if dsem is not None:
            self.dma_cum[dsem] = self.dma_cum.get(dsem, 0) + inc * ndma
            cum = self.dma_cum[dsem]
        deps.discard(i)
        self.ops.append(dict(eng=eng, fn=fn, deps=deps, dsem=dsem, cum=cum, sig=False, inc=inc))
        return i

    def selfcheck(self):
        cnt = {}
        pos = {e: 0 for e in self.trace}
        prog = True
        while prog:
            prog = False
            for e, tr in self.trace.items():
                while pos[e] < len(tr):
                    tw, ti = tr[pos[e]]
                    if all(cnt.get(k, 0) >= v for k, v in tw):
                        for k, a in ti:
                            cnt[k] = cnt.get(k, 0) + a
                        pos[e] += 1
                        prog = True
                    else:
                        break
        stuck = {e: (pos[e], len(tr)) for e, tr in self.trace.items() if pos[e] < len(tr)}
        assert not stuck, f"deadlock in sync plan: {stuck}"
        print("selfcheck ok", {e: len(t) for e, t in self.trace.items()}, "max sem", max(cnt.values()))

    def emit(self, nc, stack):
        ops = self.ops
        for o in ops:
            for d in o["deps"]:
                Dd = ops[d]
                if Dd["dsem"] is None and not (Dd["eng"] == "pe" and o["eng"] == "pe"):
                    Dd["sig"] = True
        engs = ["pe", "act", "dve", "pool", "sp"]
        esem = {}
        for e in engs:
            cnt = 0
            for o in ops:
                if o["eng"] == e and o["dsem"] is None and o["sig"]:
                    ep, v = cnt // EPOCH, cnt % EPOCH + 1
                    if (e, ep) not in esem:
                        esem[(e, ep)] = stack.enter_context(nc.semaphore(f"s_{e}_{ep}"))
                    o["semv"] = (esem[(e, ep)], v)
                    cnt += 1
        dsems = {k: stack.enter_context(nc.semaphore(f"d_{k}")) for k in self.dma_cum}
        block = stack.enter_context(nc.Block())

        self.trace = {e: [] for e in engs}

        def run(e, eng):
            seen = {}
            for o in ops:
                if o["eng"] != e:
                    continue
                tw, ti = [], []
                self.trace[e].append((tw, ti))
                need = {}
                for d in o["deps"]:
                    Dd = ops[d]
                    if Dd["dsem"] is not None:
                        sem, v = dsems[Dd["dsem"]], Dd["cum"]
                    else:
                        if Dd["eng"] == "pe" and e == "pe":
                            continue
                        sem, v = Dd["semv"]
                    key = id(sem)
                    if key not in need or need[key][1] < v:
                        need[key] = (sem, v)
                for key, (sem, v) in need.items():
                    if seen.get(key, 0) < v:
                        eng.wait_ge(sem, v)
                        seen[key] = v
                        tw.append((key, v))
                res = o["fn"](eng)
                if res is None:
                    assert not o["sig"] and o["dsem"] is None, "no-op with dependents"
                    continue
                if not isinstance(res, (list, tuple)):
                    res = [res]
                if o["dsem"] is not None:
                    for ins in res:
                        ins.then_inc(dsems[o["dsem"]], o["inc"])
                        ti.append((id(dsems[o["dsem"]]), o["inc"]))
                elif o["sig"]:
                    res[-1].then_inc(o["semv"][0], 1)
                    ti.append((id(o["semv"][0]), 1))

        @block.tensor
        def _(eng):
            run("pe", eng)

        @block.scalar
        def _(eng):
            run("act", eng)

        @block.vector
        def _(eng):
            run("dve", eng)

        @block.gpsimd
        def _(eng):
            run("pool", eng)

        @block.sync
        def _(eng):
            run("sp", eng)


def build_program():
    nc = bass.Bass("TRN2", target_bir_lowering=False)
    P = Prog()
    dt = lambda n, s, d, k: nc.dram_tensor(n, s, d, kind=k).ap()
    xin = dt("xin", [NT, D], F32, "ExternalInput")
    sconv = dt("sconv", [2, 480, D], F32, "ExternalInput")
    ck = dt("ck", [16, 2048, 512], F32, "ExternalInput")
    cv = dt("cv", [16, 2048, 512], F32, "ExternalInput")
    wg_d = dt("wg", [8, 11, 128, 2048], F32, "ExternalInput")
    wu_d = dt("wu", [8, 11, 128, 2048], F32, "ExternalInput")
    wd_d = dt("wd", [8, 8, 128, 2816], F32, "ExternalInput")
    w1_d = dt("w1", [2, 8, 128, 2048], F32, "ExternalInput")
    w2_d = dt("w2", [2, 4, 128, 2048], F32, "ExternalInput")
    wk_d = dt("wk", [2, 128, 2048], F32, "ExternalInput")
    wv_d = dt("wv", [2, 128, 2048], F32, "ExternalInput")
    wq_d = dt("wq", [2, 6, 128, 2048], F32, "ExternalInput")
    wo_d = dt("wo", [2, 4, 64, 2048], F32, "ExternalInput")
    pcol_d = dt("pcol", [128, NCOL], F32, "ExternalInput")
    prow_d = dt("prow", [128, 192], F32, "ExternalInput")
    rope_d = dt("rope", [128, 18 * 16], F32, "ExternalInput")
    ident_d = dt("ident", [128, 128], F32, "ExternalInput")
    mask_d = dt("masks", [128, 768], BF16, "ExternalInput")
    smask_d = dt("smask", [128, 33 * 24], F32, "ExternalInput")

    y_o = dt("y", [2176, D], F32, "ExternalOutput")
    scp_o = dt("scp", [2, 30, D], F32, "ExternalOutput")
    scs_o = dt("scs", [2, 16, 30, D], F32, "ExternalOutput")
    kp_o = dt("kp", [2048, 512], F32, "ExternalOutput")
    vp_o = dt("vp", [2048, 512], F32, "ExternalOutput")
    ks_o = dt("ks", [16, 2048, 512], F32, "ExternalOutput")
    vs_o = dt("vs", [16, 2048, 512], F32, "ExternalOutput")

    kvbin = nc.dram_tensor("kvbin", [4096, 512], F32)
    gath = nc.dram_tensor("gath", [8 * 4096, 512], F32)
    vloc = nc.dram_tensor("vloc", [4096, 512], BF16).ap()
    kTd = nc.dram_tensor("kTd", [512, 4224], BF16).ap()

    with ExitStack() as st:
        sb = lambda n, s, d: st.enter_context(nc.sbuf_tensor(n, s, d))
        xT = sb("xT", [128, 8, NT], F32)
        hn = sb("hn", [128, 8, 512], BF16)
        R = sb("R", [128, 10240], F32)
        rstd = sb("rstd", [128, 512], F32)
        st2 = sb("st2", [128, 3, 512], F32)
        sg = sb("sg", [128, 2, 512], BF16)
        sgf = sb("sgf", [128, 2, 512], F32)
        wgb = [sb(f"wgb{i}", [128, 8, 256], BF16) for i in range(3)]
        wub = [sb(f"wub{i}", [128, 8, 256], BF16) for i in range(3)]
        wdb = [sb(f"wdb{i}", [128, 22, 128], BF16) for i in range(2)]
        kTl = sb("kTl", [128, 4224], BF16)
        pcol = sb("pcol_s", [128, NCOL], F32)
        prow = sb("prow_s", [128, 192], F32)
        rope = sb("rope_s", [128, 18, 16], F32)
        ident = sb("ident_s", [128, 128], F32)
        identb = sb("identb", [128, 128], BF16)
        onesb = sb("onesb", [128, 128], BF16)
        onesf = sb("onesf", [128, 64], F32)
        masks = sb("masks_s", [128, 768], BF16)
        smask = sb("smask_s", [128, 33, 24], F32)
        tokst = sb("tokst", [128, 1024], F32)
        vnew = sb("vnew", [128, 512], F32)
        ssq = sb("ssq", [128, 24], F32)
        ps = [st.enter_context(nc.psum_tensor(f"ps{i}", [128, 512], F32)) for i in range(7)]
        psb = st.enter_context(nc.psum_tensor("psb", [128, 1024], BF16))
        dummy = sb("dummy_t", [128, 2], F32)

        Rb = R[:, :].bitcast(BF16)
        h2 = Rb[:, 0:11264].rearrange("p (a b) -> p a b", a=22)
        uT = R[:, 0:6144].rearrange("p (a b) -> p a b", a=8)
        cT = R[:, 6144:10240].rearrange("p (a b) -> p a b", a=8)
        us = R[:, 6144:7168].rearrange("p (a b) -> p a b", a=8)
        qsq = R[:, 0:1536]
        ktok = R[:, 1536:2560].rearrange("p (a b) -> p a b", a=2)
        kb16 = Rb[:, 5120:6656]
        vb16 = Rb[:, 6656:7168]
        rt = R[:, 3584:4352].rearrange("p (a b c) -> p a b c", a=4, b=24)
        kts = Rb[:, 8704:9216].rearrange("p (a b) -> p a b", a=4)
        qst = R[:, 8000:9536]
        QT = Rb[:, 9856:16000].rearrange("p (a b) -> p a b", a=12)
        combT = Rb[:, 16000:20096].rearrange("p (a b) -> p a b", a=8)[0:64]
        vt1 = Rb[:, 0:640].rearrange("p (a b) -> p a b", a=5)
        vt4 = Rb[:, 640:1664].rearrange("p (a b) -> p a b", a=8)
        vt16 = Rb[:, 1664:5760].rearrange("p (a b) -> p a b", a=32)
        Eb = Rb[:, 5760:6784].rearrange("p (a b) -> p a b", a=2)
        accn = R[:, 3392:3904]
        accd = R[:, 3904:4416]
        rden = R[:, 4416:4928]
        KEY = lambda *a: tuple(a)

        wslot = {"g": 0, "u": 0, "d": 0}

        def load_w(kind, src_ap, parts=128):
            if kind == "d":
                s = wslot["d"] % 2; wslot["d"] += 1
                buf = wdb[s]
                P.op("pool", lambda e: e.dma_start(out=buf[:, :, :].rearrange("p a b -> p (a b)"), in_=src_ap),
                     w=[KEY("wd", s)], dsem=f"wd{s}")
                return buf, KEY("wd", s)
            s = wslot[kind] % 3; wslot[kind] += 1
            buf = (wgb if kind == "g" else wub)[s]
            P.op("pool", lambda e: e.dma_start(out=buf[0:parts, :, :].rearrange("p a b -> p (a b)"), in_=src_ap),
                 w=[KEY("w" + kind, s)], dsem=f"w{kind}{s}")
            return buf, KEY("w" + kind, s)

        def rms_block(c0, N, gcol, xk):
            P.op("act", lambda e: e.activation(out=hn[:, :, 0:N], in_=xT[:, :, c0:c0 + N], func=AF.Square),
                 r=[xk], w=[KEY("hn", k) for k in range(8)])
            for kc in range(8):
                P.op("pe", lambda e, kc=kc: e.matmul(ps[0][:, 0:N], lhsT=onesb[:, :], rhs=hn[:, kc, 0:N],
                                                      start=(kc == 0), stop=(kc == 7)),
                     r=[KEY("hn", kc), "const"], w=[KEY("ps", 0)])
            P.op("act", lambda e: e.activation(out=rstd[:, 0:N], in_=ps[0][:, 0:N], func=AF.Sqrt, bias=EPS, scale=1.0 / D),
                 r=[KEY("ps", 0)], w=["rstd"])
            P.op("dve", lambda e: e.reciprocal(out=rstd[:, 0:N], in_=rstd[:, 0:N]), r=["rstd"], w=["rstd"])
            for kc in range(8):
                P.op("dve", lambda e, kc=kc: e.scalar_tensor_tensor(
                    out=hn[:, kc, 0:N], in0=xT[:, kc, c0:c0 + N], scalar=pcol[:, gcol + kc:gcol + kc + 1],
                    in1=rstd[:, 0:N], op0=ALU.mult, op1=ALU.mult), r=[xk, "rstd", "const"], w=[KEY("hn", kc)])

        def ffn(idx, c0, N, xk):
            R_barrier()
            rms_block(c0, N, C_FFN + idx * 8, xk)
            for pc in range(11):
                bg, kg = load_w("g", wg_d[idx, pc])
                bu, ku = load_w("u", wu_d[idx, pc])
                for j in range(2):
                    fc = 2 * pc + j
                    pg, pu = 1 + fc % 2, 3 + fc % 2
                    for kc in range(8):
                        P.op("pe", lambda e, kc=kc, bg=bg, j=j, pg=pg: e.matmul(
                            ps[pg][:, 0:N], lhsT=bg[:, kc, j * 128:(j + 1) * 128], rhs=hn[:, kc, 0:N],
                            start=(kc == 0), stop=(kc == 7)), r=[kg, KEY("hn", kc)], w=[KEY("ps", pg)])
                    for kc in range(8):
                        P.op("pe", lambda e, kc=kc, bu=bu, j=j, pu=pu: e.matmul(
                            ps[pu][:, 0:N], lhsT=bu[:, kc, j * 128:(j + 1) * 128], rhs=hn[:, kc, 0:N],
                            start=(kc == 0), stop=(kc == 7)), r=[ku, KEY("hn", kc)], w=[KEY("ps", pu)])
                    P.op("act", lambda e, pg=pg, fc=fc: e.activation(out=sg[:, fc % 2, 0:N], in_=ps[pg][:, 0:N],
                                                                     func=AF.Silu),
                         r=[KEY("ps", pg)], w=[KEY("sg", fc % 2)])
                    P.op("dve", lambda e, pu=pu, fc=fc: e.tensor_tensor(out=h2[:, fc, 0:N], in0=ps[pu][:, 0:N],
                                                                        in1=sg[:, fc % 2, 0:N], op=ALU.mult),
                         r=[KEY("ps", pu), KEY("sg", fc % 2)], w=[KEY("R", "h2", fc)], R=1)
            for dc in range(8):
                bd, kd = load_w("d", wd_d[idx, dc])
                po = 5 + dc % 2
                for fc in range(22):
                    P.op("pe", lambda e, fc=fc, bd=bd, po=po: e.matmul(
                        ps[po][:, 0:N], lhsT=bd[:, fc, :], rhs=h2[:, fc, 0:N], start=(fc == 0), stop=(fc == 21)),
                        r=[kd, KEY("R", "h2", fc)], w=[KEY("ps", po)], R=1)
                P.op("dve", lambda e, dc=dc, po=po: e.scalar_tensor_tensor(
                    out=xT[:, dc, c0:c0 + N], in0=ps[po][:, 0:N], scalar=0.5, in1=xT[:, dc, c0:c0 + N],
                    op0=ALU.mult, op1=ALU.add), r=[KEY("ps", po), xk], w=[xk])

        tailb = sb("tailb", [128, 2, 240], F32)
        tail_pending = {}

        def R_barrier():
            P.op("dve", lambda e: e.memset(dummy[:, 0:1], 0.0), r=[], w=["Rrole", "dummyk"])

        def transpose_out(src_fn, nrow, dst_dma_fn, rkeys, tag):
            for half in range(2):
                pb = 5 + half
                for q in range(4):
                    dc = half * 4 + q
                    P.op("pe", lambda e, dc=dc, pb=pb, q=q: e.transpose(ps[pb][:, q * 128:(q + 1) * 128], src_fn(dc),
                                                                        ident[:, :]),
                         r=rkeys + ["const"], w=[KEY("ps", pb)])
                P.op("act", lambda e, pb=pb, half=half: e.activation(out=tokst[:, half * 512:(half + 1) * 512],
                                                                     in_=ps[pb][:, :], func=AF.Copy),
                     r=[KEY("ps", pb)], w=["tokst"])
            dst_dma_fn()

        def ld(dst, src, key="const"):
            P.op("sp", lambda e: e.dma_start(out=dst, in_=src), w=[key], dsem="cst")
        ld(pcol[:, :], pcol_d[:, :]); ld(prow[:, :], prow_d[:, :])
        ld(rope[:, :, :].rearrange("p a b -> p (a b)"), rope_d[:, :]); ld(ident[:, :], ident_d[:, :])
        ld(masks[:, :], mask_d[:, :]); ld(smask[:, :, :].rearrange("p a b -> p (a b)"), smask_d[:, :])
        P.op("dve", lambda e: e.tensor_copy(out=identb[:, :], in_=ident[:, :]), r=["const"], w=["const2"])
        P.op("dve", lambda e: e.memset(onesb[:, :], 1.0), w=["const3"])
        P.op("dve", lambda e: e.memset(onesf[:, :], 1.0), w=["const4"])
        P.op("dve", lambda e: e.memset(dummy[:, 1:2], 0.0), r=["const", "const2", "const3", "const4"], w=["const"])

        for tt in range(18):
            P.op("sp", lambda e, tt=tt: e.dma_start(out=tokst[:, :], in_=xin[tt * 128:(tt + 1) * 128, :]),
                 w=["tokst"], dsem="xin")
            for half in range(2):
                pb = 5 + half
                for q in range(4):
                    dc = half * 4 + q
                    P.op("pe", lambda e, dc=dc, pb=pb, q=q: e.transpose(ps[pb][:, q * 128:(q + 1) * 128],
                                                                        tokst[:, dc * 128:(dc + 1) * 128], ident[:, :]),
                         r=["tokst", "const"], w=[KEY("ps", pb)])
                P.op("act", lambda e, pb=pb, half=half, tt=tt: e.activation(
                    out=xT[:, half * 4:(half + 1) * 4, tt * 128:(tt + 1) * 128],
                    in_=ps[pb][:, :].rearrange("p (a b) -> p a b", a=4), func=AF.Copy),
                    r=[KEY("ps", pb)], w=[KEY("x", blk_of_col(tt * 128))])

        for b in range(16):
            P.op("sp", lambda e, b=b: e.dma_start(out=ks_o[b, 0:2040, :], in_=ck[b, 8:2048, :]), w=[KEY("kso", b)], dsem="oc")
            P.op("sp", lambda e, b=b: e.dma_start(out=vs_o[b, 0:2040, :], in_=cv[b, 8:2048, :]), w=[KEY("vso", b)], dsem="oc")
        for l in range(2):
            P.op("sp", lambda e, l=l: e.dma_start(
                out=scs_o[l, :, 0:22, :], in_=sconv[l].rearrange("(b r) d -> b r d", r=30)[:, 8:30, :]),
                w=[KEY("scso", l)], dsem="oc")

        def conv_module(l, bi, c0, N, xk):
            first = (bi == 0)
            last = (bi == 4)
            R_barrier()
            if tail_pending.get(l):
                P.op("pool", lambda e: e.tensor_copy(out=uT[:, :, 0:30], in_=tailb[:, l, :].rearrange("p (c r) -> p c r", r=30)),
                     r=[KEY("tail", l)], w=["uT"], R=1)
                tail_pending[l] = False
            rms_block(c0, N, C_CN + l * 8, xk)
            off = 608 if first else 0
            NS = N - 128 if first else N
            sc0 = 128 if first else 0
            if first:
                for g4 in range(4):
                    P.op("sp", lambda e, g4=g4: e.dma_start(out=tokst[0:120, :], in_=sconv[l, g4 * 120:(g4 + 1) * 120, :]),
                         w=["tokst"], dsem="xin")
                    for half in range(2):
                        pb = 5 + half
                        for q in range(4):
                            dc = half * 4 + q
                            P.op("pe", lambda e, dc=dc, pb=pb, q=q: e.transpose(
                                ps[pb][:, q * 128:q * 128 + 120], tokst[0:120, dc * 128:(dc + 1) * 128], ident[0:120, 0:120]),
                                r=["tokst", "const"], w=[KEY("ps", pb)])
                        P.op("act", lambda e, pb=pb, half=half, g4=g4: e.activation(
                            out=uT[:, half * 4:(half + 1) * 4, 0:608].rearrange("p a (b r) -> p a b r", r=38)[:, :, g4 * 4:(g4 + 1) * 4, 0:30],
                            in_=ps[pb][:, :].rearrange("p (a b) -> p a b", a=4)[:, :, 0:120].rearrange("p a (b r) -> p a b r", r=30),
                            func=AF.Copy), r=[KEY("ps", pb)], w=["uT"])
                P.op("dve", lambda e: e.memset(uT[:, :, 608:638], 0.0), w=["uT"])
            for dc in range(8):
                bw, kw = load_w("g", w1_d[l, dc])
                p1, p2 = 1 + dc % 2, 3 + dc % 2
                for kc in range(8):
                    P.op("pe", lambda e, kc=kc, bw=bw, p1=p1: e.matmul(ps[p1][:, 0:N], lhsT=bw[:, kc, 0:128], rhs=hn[:, kc, 0:N],
                                                                start=(kc == 0), stop=(kc == 7)), r=[kw, KEY("hn", kc)], w=[KEY("ps", p1)])
                for kc in range(8):
                    P.op("pe", lambda e, kc=kc, bw=bw, p2=p2: e.matmul(ps[p2][:, 0:N], lhsT=bw[:, kc, 128:256], rhs=hn[:, kc, 0:N],
                                                                start=(kc == 0), stop=(kc == 7)), r=[kw, KEY("hn", kc)], w=[KEY("ps", p2)])
                cb = C_B1 + l * 16
                P.op("act", lambda e, dc=dc, p2=p2: e.activation(out=sgf[:, dc % 2, 0:N], in_=ps[p2][:, 0:N], func=AF.Sigmoid,
                                                                 bias=pcol[:, cb + 8 + dc:cb + 9 + dc], scale=1.0),
                     r=[KEY("ps", p2), "const"], w=[KEY("sgf", dc % 2)])
                if first:
                    P.op("dve", lambda e, dc=dc, p1=p1: e.scalar_tensor_tensor(
                        out=us[:, dc, 0:128], in0=ps[p1][:, 0:128], scalar=pcol[:, cb + dc:cb + dc + 1],
                        in1=sgf[:, dc % 2, 0:128], op0=ALU.add, op1=ALU.mult),
                        r=[KEY("ps", p1), KEY("sgf", dc % 2), "const"], w=["us"])
                    P.op("dve", lambda e, dc=dc: e.tensor_copy(
                        out=uT[:, dc, 0:608].rearrange("p (b r) -> p b r", r=38)[:, :, 30:38],
                        in_=us[:, dc, 0:128].rearrange("p (b t) -> p b t", t=8)), r=["us"], w=["uT"])
                P.op("dve", lambda e, dc=dc, p1=p1: e.scalar_tensor_tensor(
                    out=uT[:, dc, off + 30:off + 30 + NS], in0=ps[p1][:, sc0:sc0 + NS], scalar=pcol[:, cb + dc:cb + dc + 1],
                    in1=sgf[:, dc % 2, sc0:sc0 + NS], op0=ALU.add, op1=ALU.mult),
                    r=[KEY("ps", p1), KEY("sgf", dc % 2), "const"], w=["uT"])
                if first:
                    P.op("dve", lambda e, dc=dc: e.tensor_scalar(
                        out=uT[:, dc, off + 30:off + 30 + NS], in0=uT[:, dc, off + 30:off + 30 + NS],
                        scalar1=pcol[:, C_FLAG:C_FLAG + 1], scalar2=None, op0=ALU.mult), r=["uT", "const"], w=["uT"])
            if first:
                def dma_s(l=l):
                    for b in range(16):
                        P.op("sp", lambda e, b=b: e.dma_start(out=scs_o[l, b, 22:30, :], in_=tokst[b * 8:(b + 1) * 8, :]),
                             r=["tokst"], w=[KEY("scso2", l, b)], dsem="tk")
                transpose_out(lambda dc: us[:, dc, 0:128], 128, dma_s, ["us"], "scs")
            if last:
                def dma_p(l=l):
                    P.op("sp", lambda e: e.dma_start(out=scp_o[l, :, :], in_=tokst[98:128, :]),
                         r=["tokst"], w=[KEY("scpo", l)], dsem="tk")
                transpose_out(lambda dc: uT[:, dc, 30 + 384:30 + 512], 128, dma_p, ["uT"], "scp")
            dwc = C_DW + l * 248
            engs = ["dve"] * 8
            for k in range(31):
                for dc in range(8):
                    col = pcol[:, dwc + k * 8 + dc:dwc + k * 8 + dc + 1]
                    bcol = pcol[:, C_DWB + l * 8 + dc:C_DWB + l * 8 + dc + 1]
                    outs, ins = [], []
                    if first:
                        outs.append(cT[:, dc, 0:128].rearrange("p (b t) -> p b t", t=8))
                        ins.append(uT[:, dc, 0:608].rearrange("p (b r) -> p b r", r=38)[:, :, k:k + 8])
                    outs.append(cT[:, dc, sc0:sc0 + NS])
                    ins.append(uT[:, dc, off + k:off + k + NS])
                    for o_, i_ in zip(outs, ins):
                        if k == 0:
                            P.op(engs[dc], lambda e, o_=o_, i_=i_, col=col, bcol=bcol: e.tensor_scalar(
                                out=o_, in0=i_, scalar1=col, scalar2=bcol, op0=ALU.mult, op1=ALU.add),
                                r=["uT", "us", "const"], w=[KEY("cT", dc)] + (["us"] if (first and dc < 2) else []))
                        else:
                            P.op(engs[dc], lambda e, o_=o_, i_=i_, col=col: e.scalar_tensor_tensor(
                                out=o_, in0=i_, scalar=col, in1=o_, op0=ALU.mult, op1=ALU.add),
                                r=["uT", "const"], w=[KEY("cT", dc)])
            if not last:
                P.op("pool", lambda e: e.tensor_copy(out=tailb[:, l, :].rearrange("p (c r) -> p c r", r=30),
                                                     in_=uT[:, :, off + NS:off + NS + 30]),
                     r=["uT"], w=[KEY("tail", l)], R=1)
            ckeys = [KEY("cT", dc) for dc in range(8)]
            cb16 = h2
            P.op("act", lambda e: e.activation(out=hn[:, :, 0:N], in_=cT[:, :, 0:N], func=AF.Copy), r=ckeys,
                 w=[KEY("hn", k) for k in range(8)])
            for kc in range(8):
                P.op("pe", lambda e, kc=kc: e.matmul(ps[0][:, 0:N], lhsT=onesb[:, :], rhs=hn[:, kc, 0:N], start=(kc == 0), stop=(kc == 7)),
                     r=[KEY("hn", kc), "const"], w=[KEY("ps", 0)])
            P.op("dve", lambda e: e.tensor_scalar(out=st2[:, 1, 0:N], in0=ps[0][:, 0:N], scalar1=1.0 / D, scalar2=None, op0=ALU.mult),
                 r=[KEY("ps", 0)], w=["mean"])
            P.op("act", lambda e: e.activation(out=hn[:, :, 0:N], in_=cT[:, :, 0:N], func=AF.Square), r=ckeys,
                 w=[KEY("hn", k) for k in range(8)])
            for kc in range(8):
                P.op("pe", lambda e, kc=kc: e.matmul(ps[0][:, 0:N], lhsT=onesb[:, :], rhs=hn[:, kc, 0:N], start=(kc == 0), stop=(kc == 7)),
                     r=[KEY("hn", kc), "const"], w=[KEY("ps", 0)])
            P.op("dve", lambda e: e.tensor_tensor(out=st2[:, 2, 0:N], in0=st2[:, 1, 0:N], in1=st2[:, 1, 0:N], op=ALU.mult),
                 r=["mean"], w=["msq"])
            P.op("dve", lambda e: e.scalar_tensor_tensor(out=rstd[:, 0:N], in0=ps[0][:, 0:N], scalar=1.0 / D, in1=st2[:, 2, 0:N],
                                                         op0=ALU.mult, op1=ALU.subtract), r=[KEY("ps", 0), "msq"], w=["rstd"])
            P.op("act", lambda e: e.activation(out=rstd[:, 0:N], in_=rstd[:, 0:N], func=AF.Sqrt, bias=EPS, scale=1.0),
                 r=["rstd"], w=["rstd"])
            P.op("dve", lambda e: e.reciprocal(out=rstd[:, 0:N], in_=rstd[:, 0:N]), r=["rstd"], w=["rstd"])
            P.op("dve", lambda e: e.tensor_tensor(out=cT[:, :, 0:N], in0=cT[:, :, 0:N],
                                                  in1=st2[:, 1:2, 0:N].to_broadcast([128, 8, N]), op=ALU.subtract),
                 r=["mean"] + ckeys, w=ckeys)
            P.op("dve", lambda e: e.tensor_tensor(out=cT[:, :, 0:N], in0=cT[:, :, 0:N],
                                                  in1=rstd[:, 0:N].unsqueeze(1).to_broadcast([128, 8, N]), op=ALU.mult),
                 r=["rstd"] + ckeys, w=ckeys)
            for dc in range(8):
                P.op("act", lambda e, dc=dc: e.activation(out=hn[:, dc, 0:N], in_=cT[:, dc, 0:N], func=AF.Silu,
                                                          bias=pcol[:, C_LNB + l * 8 + dc:C_LNB + l * 8 + dc + 1],
                                                          scale=pcol[:, C_LNG + l * 8 + dc:C_LNG + l * 8 + dc + 1]),
                     r=[KEY("cT", dc), "const"], w=[KEY("hn", dc)])
            for pc in range(4):
                bw, kw = load_w("u", w2_d[l, pc])
                for j in range(2):
                    dc = 2 * pc + j
                    po = 5 + dc % 2
                    for kc in range(8):
                        P.op("pe", lambda e, kc=kc, bw=bw, j=j, po=po: e.matmul(
                            ps[po][:, 0:N], lhsT=bw[:, kc, j * 128:(j + 1) * 128], rhs=hn[:, kc, 0:N],
                            start=(kc == 0), stop=(kc == 7)), r=[kw, KEY("hn", kc)], w=[KEY("ps", po)])
                    P.op("dve", lambda e, dc=dc, po=po: e.scalar_tensor_tensor(
                        out=xT[:, dc, c0:c0 + N], in0=ps[po][:, 0:N], scalar=pcol[:, C_B2 + l * 8 + dc:C_B2 + l * 8 + dc + 1],
                        in1=xT[:, dc, c0:c0 + N], op0=ALU.add, op1=ALU.add), r=[KEY("ps", po), xk, "const"], w=[xk])
            if not last:
                tail_pending[l] = True

        def norm_rope(src_banks, nh, gro, rtile, out_f32, out_b16, tagk):
            nb = len(src_banks)
            W = nh * 64
            for bi_, pb in enumerate(src_banks):
                P.op("act", lambda e, bi_=bi_, pb=pb: e.activation(out=qst[:, bi_ * 512:(bi_ + 1) * 512], in_=ps[pb][:, :], func=AF.Copy),
                     r=[KEY("ps", pb)], w=["qst"])
            P.op("act", lambda e: e.activation(out=qsq[:, 0:W], in_=qst[:, 0:W], func=AF.Square), r=["qst"], w=["qsq"])
            P.op("dve", lambda e: e.tensor_reduce(out=ssq[:, 0:nh], in_=qsq[:, 0:W].rearrange("p (h d) -> p h d", d=64),
                                                  axis=AX.X, op=ALU.add), r=["qsq"], w=["ssq"])
            P.op("act", lambda e: e.activation(out=ssq[:, 0:nh], in_=ssq[:, 0:nh], func=AF.Sqrt, bias=EPS, scale=1.0 / 64),
                 r=["ssq"], w=["ssq"])
            P.op("dve", lambda e: e.reciprocal(out=ssq[:, 0:nh], in_=ssq[:, 0:nh]), r=["ssq"], w=["ssq"])
            q3 = qst[:, 0:W].rearrange("p (h d) -> p h d", d=64)
            P.op("dve", lambda e: e.tensor_tensor(out=q3, in0=q3, in1=ssq[:, 0:nh].unsqueeze(2).to_broadcast([128, nh, 64]), op=ALU.mult),
                 r=["ssq", "qst"], w=["qst"])
            P.op("dve", lambda e: e.tensor_tensor(out=q3, in0=q3, in1=prow[:, gro:gro + 64].unsqueeze(1).to_broadcast([128, nh, 64]), op=ALU.mult),
                 r=["qst", "const"], w=["qst"])
            cosb = rope[:, rtile, 0:8].unsqueeze(1).to_broadcast([128, nh, 8])
            sinb = rope[:, rtile, 8:16].unsqueeze(1).to_broadcast([128, nh, 8])
            x1, x2 = q3[:, :, 0:8], q3[:, :, 8:16]
            P.op("dve", lambda e: e.tensor_tensor(out=rt[:, 0, 0:nh, :], in0=x1, in1=cosb, op=ALU.mult), r=["qst", "const"], w=["rt0"])
            P.op("dve", lambda e: e.tensor_tensor(out=rt[:, 1, 0:nh, :], in0=x2, in1=sinb, op=ALU.mult), r=["qst", "const"], w=["rt1"])
            P.op("dve", lambda e: e.tensor_tensor(out=rt[:, 2, 0:nh, :], in0=x2, in1=cosb, op=ALU.mult), r=["qst", "const"], w=["rt2"])
            P.op("dve", lambda e: e.tensor_tensor(out=rt[:, 3, 0:nh, :], in0=x1, in1=sinb, op=ALU.mult), r=["qst", "const"], w=["rt3"])
            P.op("dve", lambda e: e.tensor_tensor(out=x1, in0=rt[:, 0, 0:nh, :], in1=rt[:, 1, 0:nh, :], op=ALU.subtract),
                 r=["rt0", "rt1", "rt2", "rt3", "qst"], w=["qst"])
            P.op("dve", lambda e: e.tensor_tensor(out=x2, in0=rt[:, 2, 0:nh, :], in1=rt[:, 3, 0:nh, :], op=ALU.add),
                 r=["rt2", "rt3", "qst"], w=["qst"])
            P.op("act", lambda e: e.activation(out=out_b16, in_=qst[:, 0:W], func=AF.Copy), r=["qst"], w=[tagk])


        blkA = [(0, 256)] + [(256 + 512 * k, 512) for k in range(4)]
        for bi, (c0, N) in enumerate(blkA):
            if STAGE < 1 or (STAGE < 2 and bi > 0):
                break
            xk = KEY("x", bi)
            for l in range(2):
                ffn(l * 2 + 0, c0, N, xk)
                conv_module(l, bi, c0, N, xk)
                ffn(l * 2 + 1, c0, N, xk)
            R_barrier()
            rms_block(c0, N, C_KVN, xk)
            bk = [load_w("g", wk_d[j]) for j in range(2)]
            bv = [load_w("u", wv_d[j]) for j in range(2)]
            tiles = [0] if bi == 0 else list(range(4))
            for tt in tiles:
                gt = (c0 // 128) + tt
                for (bufs, pb) in ((bk, 1), (bv, 3)):
                    for j in range(2):
                        bw, kw = bufs[j]
                        for kc in range(8):
                            P.op("pe", lambda e, kc=kc, bw=bw, j=j, pb=pb, tt=tt: e.matmul(
                                ps[pb][:, j * 256:(j + 1) * 256], lhsT=hn[:, kc, tt * 128:(tt + 1) * 128], rhs=bw[:, kc, :],
                                start=(kc == 0), stop=(kc == 7)), r=[kw, KEY("hn", kc)], w=[KEY("ps", pb)])
                P.op("act", lambda e: e.activation(out=ktok[:, 1, :], in_=ps[3][:, :], func=AF.Copy), r=[KEY("ps", 3)], w=["vtok"])
                P.op("dve", lambda e: e.tensor_copy(out=vb16[:, :], in_=ktok[:, 1, :]), r=["vtok"], w=["vb16"])
                norm_rope([1], 8, 0, gt, None, kb16[:, 0:512], "kb16")
                P.op("dve", lambda e: e.tensor_copy(out=ktok[:, 0, :], in_=qst[:, 0:512]), r=["qst"], w=["ktokk"])
                for c4 in range(4):
                    P.op("pe", lambda e, c4=c4: e.transpose(psb[:, c4 * 128:(c4 + 1) * 128], kb16[:, c4 * 128:(c4 + 1) * 128], identb[:, :]),
                         r=["kb16", "const"], w=["psb"])
                kcol = 4096 if bi == 0 else 2048 + (c0 - 256) + tt * 128
                P.op("act", lambda e: e.activation(out=kts[:, :, :], in_=psb[:, 0:512].rearrange("p (a b) -> p a b", a=4), func=AF.Copy),
                     r=["psb"], w=["kts"])
                P.op("sp", lambda e, kcol=kcol: e.dma_start(out=kTd.rearrange("(c p) t -> p c t", p=128)[:, :, kcol:kcol + 128], in_=kts[:, :, :]),
                     r=["kts"], w=["kTd"], dsem="stg")
                if bi == 0:
                    P.op("dve", lambda e: e.tensor_copy(out=vnew[:, :], in_=ktok[:, 1, :]), r=["vtok"], w=["vnew"])
                    for b in range(16):
                        P.op("sp", lambda e, b=b: e.dma_start(out=ks_o[b, 2040:2048, :], in_=ktok[b * 8:(b + 1) * 8, 0, :]),
                             r=["ktokk"], w=[KEY("kso2", b)], dsem="stg")
                        P.op("sp", lambda e, b=b: e.dma_start(out=vs_o[b, 2040:2048, :], in_=ktok[b * 8:(b + 1) * 8, 1, :]),
                             r=["vtok"], w=[KEY("vso2", b)], dsem="stg")
                else:
                    t0 = (c0 - 256) + tt * 128
                    P.op("sp", lambda e, t0=t0: e.dma_start(out=kp_o[t0:t0 + 128, :], in_=ktok[:, 0, :]), r=["ktokk"], w=[KEY("kpo", t0)], dsem="stg")
                    P.op("sp", lambda e, t0=t0: e.dma_start(out=vp_o[t0:t0 + 128, :], in_=ktok[:, 1, :]), r=["vtok"], w=[KEY("vpo", t0)], dsem="stg")
                    P.op("sp", lambda e, t0=t0: e.dma_start(out=kvbin.ap()[t0:t0 + 128, :], in_=ktok[:, 0, :]), r=["ktokk"], w=["kvbin"], dsem="stg")
                    P.op("sp", lambda e, t0=t0: e.dma_start(out=kvbin.ap()[2048 + t0:2048 + t0 + 128, :], in_=ktok[:, 1, :]), r=["vtok"], w=["kvbin"], dsem="stg")
                    P.op("sp", lambda e, t0=t0: e.dma_start(out=vloc[2048 + t0:2048 + t0 + 128, :], in_=vb16[:, :]), r=["vb16"], w=["vloc"], dsem="stg")

        if STAGE >= 3:
            P.op("pool", lambda e: e.collective_compute("AllGather", ALU.bypass, replica_groups=[list(range(8))],
                                                         ins=[kvbin.ap().opt()], outs=[gath.ap().opt()]),
                 r=["kvbin"], w=["gath"], dsem="cc", inc=1)
            R_barrier()
            for tt in range(16):
                for r_ in range(8):
                    P.op("sp", lambda e, tt=tt, r_=r_: e.dma_start(
                        out=qst[:, 0:1024].rearrange("p (a b) -> p a b", a=2),
                        in_=gath.ap()[r_ * 4096:(r_ + 1) * 4096, :].rearrange("(a t) f -> t a f", a=2)[tt * 128:(tt + 1) * 128, :, :]),
                        r=["gath"], w=["qst"], dsem="gl")
                    if r_ == 0:
                        P.op("dve", lambda e, r_=r_: e.tensor_scalar(out=qsq[:, 0:1024], in0=qst[:, 0:1024],
                                                                     scalar1=pcol[:, C_SEL + r_:C_SEL + r_ + 1], scalar2=None, op0=ALU.mult),
                             r=["qst", "const"], w=["qsq"])
                    else:
                        P.op("dve", lambda e, r_=r_: e.scalar_tensor_tensor(out=qsq[:, 0:1024], in0=qst[:, 0:1024],
                                                                            scalar=pcol[:, C_SEL + r_:C_SEL + r_ + 1], in1=qsq[:, 0:1024],
                                                                            op0=ALU.mult, op1=ALU.add), r=["qst", "qsq", "const"], w=["qsq"])
                P.op("act", lambda e: e.activation(out=kb16[:, 0:512], in_=qsq[:, 0:512], func=AF.Copy), r=["qsq"], w=["kb16"])
                P.op("act", lambda e: e.activation(out=vb16[:, :], in_=qsq[:, 512:1024], func=AF.Copy), r=["qsq"], w=["vb16"])
                for c4 in range(4):
                    P.op("pe", lambda e, c4=c4: e.transpose(psb[:, c4 * 128:(c4 + 1) * 128], kb16[:, c4 * 128:(c4 + 1) * 128], identb[:, :]),
                         r=["kb16", "const"], w=["psb"])
                P.op("act", lambda e: e.activation(out=kts[:, :, :], in_=psb[:, 0:512].rearrange("p (a b) -> p a b", a=4), func=AF.Copy),
                     r=["psb"], w=["kts"])
                P.op("sp", lambda e, tt=tt: e.dma_start(out=kTd.rearrange("(c p) t -> p c t", p=128)[:, :, tt * 128:(tt + 1) * 128], in_=kts[:, :, :]),
                     r=["kts"], w=["kTd"], dsem="stg")
                P.op("sp", lambda e, tt=tt: e.dma_start(out=vloc[tt * 128:(tt + 1) * 128, :], in_=vb16[:, :]), r=["vb16"], w=["vloc"], dsem="stg")

        def attention(i, bi, c0, N, xk):
            R_barrier()
            rms_block(c0, N, C_AN + i * 8, xk)
            qb = []
            for pc in range(6):
                qb.append(load_w("g" if pc < 3 else "u", wq_d[i, pc]))
            for tt in range(N // 128):
                gt = (c0 // 128) + tt
                for pc in range(6):
                    bw, kw = qb[pc]
                    pb = 1 + pc // 2
                    for kc in range(8):
                        P.op("pe", lambda e, kc=kc, bw=bw, pb=pb, pc=pc, tt=tt: e.matmul(
                            ps[pb][:, (pc % 2) * 256:(pc % 2 + 1) * 256], lhsT=hn[:, kc, tt * 128:(tt + 1) * 128], rhs=bw[:, kc, :],
                            start=(kc == 0), stop=(kc == 7)), r=[kw, KEY("hn", kc)], w=[KEY("ps", pb)])
                norm_rope([1, 2, 3], 24, 64 + i * 64, gt, None, kb16[:, 0:1536], "kb16")
                for rr in range(2):
                    n4 = 8 if rr == 0 else 4
                    for c4 in range(n4):
                        cc = rr * 8 + c4
                        P.op("pe", lambda e, c4=c4, cc=cc: e.transpose(psb[:, c4 * 128:(c4 + 1) * 128], kb16[:, cc * 128:(cc + 1) * 128], identb[:, :]),
                             r=["kb16", "const"], w=["psb"])
                    P.op("act", lambda e, rr=rr, n4=n4, tt=tt: e.activation(
                        out=QT[:, rr * 8:rr * 8 + n4, tt * 128:(tt + 1) * 128],
                        in_=psb[:, 0:n4 * 128].rearrange("p (a b) -> p a b", a=n4), func=AF.Copy),
                        r=["psb"], w=["QT"])
            R_barrier()
            if bi == 0:
                sample_core(i)
            else:
                prompt_core(i, bi - 1)
            cN = N
            for pc in range(4):
                bw, kw = load_w("u", wo_d[i, pc], parts=64)
                for j in range(2):
                    dc = 2 * pc + j
                    po = 5 + dc % 2
                    for h in range(8):
                        P.op("pe", lambda e, h=h, bw=bw, j=j, po=po: e.matmul(
                            ps[po][:, 0:cN], lhsT=bw[0:64, h, j * 128:(j + 1) * 128], rhs=combv(h, cN),
                            start=(h == 0), stop=(h == 7)), r=[kw, "comb"], w=[KEY("ps", po)])
                    P.op("dve", lambda e, dc=dc, po=po: e.tensor_tensor(out=xT[:, dc, c0:c0 + N], in0=ps[po][:, 0:N],
                                                                        in1=xT[:, dc, c0:c0 + N], op=ALU.add),
                         r=[KEY("ps", po), xk], w=[xk])


        def combv(h, n):
            return combT[:, h, 0:n]

        def prompt_core(i, k):
            hasprev_tile_k0 = (k == 0)
            for hp in range(4):
                cs_ = slice(hp * 128, (hp + 1) * 128)
                P.op("sp", lambda e, hp=hp: e.dma_start(out=kTl[:, :], in_=kTd[hp * 128:(hp + 1) * 128, :]), r=["kTd"], w=["kTl"], dsem="ktl")
                r1 = 2048 + 128 * (4 * k - 1)
                P.op("sp", lambda e, r1=r1, cs_=cs_: e.dma_start(out=vt1[:, :, :], in_=vloc[r1:r1 + 640, cs_].rearrange("(n i) c -> i n c", i=128)),
                     r=["vloc"], w=["vt"], dsem="vt")
                r4 = 2048 + 512 * (k - 1)
                for n_ in range(2):
                    P.op("sp", lambda e, r4=r4, cs_=cs_, n_=n_: e.dma_start(
                        out=vt4[:, n_ * 4:(n_ + 1) * 4, :], in_=vloc[r4 + n_ * 512:r4 + (n_ + 1) * 512, cs_].rearrange("(i r) c -> i r c", r=4)),
                        r=["vloc"], w=["vt"], dsem="vt")
                for c_ in range(2):
                    for r0 in range(0, 16, 4):
                        P.op("sp", lambda e, cs_=cs_, c_=c_, r0=r0: e.dma_start(
                            out=vt16[:, c_ * 16 + r0:c_ * 16 + r0 + 4, :],
                            in_=vloc[c_ * 2048:(c_ + 1) * 2048, cs_].rearrange("(i r) c -> i r c", r=16)[:, r0:r0 + 4, :]),
                            r=["vloc"], w=["vt"], dsem="vt")
                for hh in range(2):
                    h = hp * 2 + hh
                    prow_ = slice(hh * 64, hh * 64 + 64)
                    vcol = slice(hh * 64, hh * 64 + 64)

                    def ktile(dil, start):
                        s = 2048 + start
                        return kTl[prow_, s:s + dil * 127 + 1:dil]
                    sbank = [0]

                    def group(g, dil, nsub, nq):
                        qchunk = g * 4 + hp
                        per_bank = 512 // (2 * nq)
                        nbank = nsub // per_bank
                        for bnk in range(nbank):
                            pbk = 1 + (sbank[0] % 2); sbank[0] += 1
                            eb = sbank[0] % 2
                            subs = range(bnk * per_bank, (bnk + 1) * per_bank)
                            for si, s_ in enumerate(subs):
                                if dil == 1:
                                    qs = slice(s_ * 128, s_ * 128 + 128)
                                    kcur = 512 * k + 128 * s_; kprev = kcur - 128
                                elif dil == 4:
                                    qs = slice(s_, s_ + 4 * 127 + 1, 4)
                                    kcur = 512 * k + s_; kprev = kcur - 512
                                else:
                                    qs = slice(s_, s_ + 16 * 31 + 1, 16)
                                    kcur = s_; kprev = s_ - 2048
                                for kt_, kst in enumerate((kprev, kcur)):
                                    lk_ = ktile(dil, kst)
                                    rq_ = QT[prow_, qchunk, qs]
                                    P.op("pe", lambda e, pbk=pbk, si=si, kt_=kt_, nq=nq, lk_=lk_, rq_=rq_: e.matmul(
                                        ps[pbk][:, (si * 2 + kt_) * nq:(si * 2 + kt_ + 1) * nq], lhsT=lk_,
                                        rhs=rq_, start=True, stop=True),
                                        r=["kTl", "QT"], w=[KEY("ps", pbk)])
                            P.op("act", lambda e, pbk=pbk, eb=eb: e.activation(out=Eb[:, eb, :], in_=ps[pbk][:, :], func=AF.Exp, scale=0.125),
                                 r=[KEY("ps", pbk)], w=[KEY("E", eb)])
                            if dil == 16:
                                mk = masks[:, 512 + 64 * k:512 + 64 * k + 64].unsqueeze(1).to_broadcast([128, per_bank, 64])
                                P.op("dve", lambda e, eb=eb, mk=mk, per_bank=per_bank: e.tensor_tensor(
                                    out=Eb[:, eb, :].rearrange("p (a b) -> p a b", a=per_bank),
                                    in0=Eb[:, eb, :].rearrange("p (a b) -> p a b", a=per_bank), in1=mk, op=ALU.mult),
                                    r=[KEY("E", eb), "const"], w=[KEY("E", eb)])
                            else:
                                for si, s_ in enumerate(subs):
                                    special = (k == 0) and ((dil == 1 and s_ == 0) or dil == 4)
                                    mo = 256 if special else 0
                                    P.op("dve", lambda e, eb=eb, si=si, mo=mo: e.tensor_tensor(
                                        out=Eb[:, eb, si * 256:(si + 1) * 256], in0=Eb[:, eb, si * 256:(si + 1) * 256],
                                        in1=masks[:, mo:mo + 256], op=ALU.mult), r=[KEY("E", eb), "const"], w=[KEY("E", eb)])
                            for si, s_ in enumerate(subs):
                                if dil == 1:
                                    vts = (vt1[:, s_, vcol], vt1[:, s_ + 1, vcol])
                                elif dil == 4:
                                    vts = (vt4[:, s_, vcol], vt4[:, 4 + s_, vcol])
                                else:
                                    vts = (vt16[:, s_, vcol], vt16[:, 16 + s_, vcol])
                                oc = slice(s_ * nq, (s_ + 1) * nq)
                                for kt_ in range(2):
                                    P.op("pe", lambda e, kt_=kt_, si=si, oc=oc, eb=eb, vts=vts, nq=nq: e.matmul(
                                        ps[3][0:64, oc], lhsT=vts[kt_], rhs=Eb[:, eb, (si * 2 + kt_) * nq:(si * 2 + kt_ + 1) * nq],
                                        start=(kt_ == 0), stop=(kt_ == 1)), r=[KEY("E", eb), "vt"], w=[KEY("ps", 3)])
                                for kt_ in range(2):
                                    P.op("pe", lambda e, kt_=kt_, si=si, oc=oc, eb=eb, nq=nq: e.matmul(
                                        ps[4][0:64, oc], lhsT=onesb[:, 0:64], rhs=Eb[:, eb, (si * 2 + kt_) * nq:(si * 2 + kt_ + 1) * nq],
                                        start=(kt_ == 0), stop=(kt_ == 1)), r=[KEY("E", eb), "const"], w=[KEY("ps", 4)])
                        if dil == 1:
                            P.op("act", lambda e: e.activation(out=accn[0:64, :], in_=ps[3][0:64, :], func=AF.Copy), r=[KEY("ps", 3)], w=["accn"])
                            P.op("act", lambda e: e.activation(out=accd[0:64, :], in_=ps[4][0:64, :], func=AF.Copy), r=[KEY("ps", 4)], w=["accd"])
                        else:
                            an = accn[0:64, :].rearrange("p (i r) -> p r i", r=dil)
                            ad = accd[0:64, :].rearrange("p (i r) -> p r i", r=dil)
                            P.op("dve", lambda e, an=an, dil=dil: e.tensor_tensor(out=an, in0=ps[3][0:64, :].rearrange("p (r i) -> p r i", r=dil), in1=an, op=ALU.add),
                                 r=[KEY("ps", 3), "accn"], w=["accn"])
                            P.op("dve", lambda e, ad=ad, dil=dil: e.tensor_tensor(out=ad, in0=ps[4][0:64, :].rearrange("p (r i) -> p r i", r=dil), in1=ad, op=ALU.add),
                                 r=[KEY("ps", 4), "accd"], w=["accd"])
                    group(0, 1, 4, 128)
                    group(1, 4, 4, 128)
                    group(2, 16, 16, 32)
                    P.op("dve", lambda e: e.reciprocal(out=rden[0:64, :], in_=accd[0:64, :]), r=["accd"], w=["rden"])
                    P.op("dve", lambda e, h=h: e.tensor_tensor(out=combT[:, h, :], in0=accn[0:64, :], in1=rden[0:64, :], op=ALU.mult),
                         r=["accn", "rden"], w=["comb"])

        skt = R[:, 0:1024].rearrange("p (a b) -> p a b", a=2)
        svt = R[:, 1024:2048].rearrange("p (a b) -> p a b", a=2)
        skT = Rb[:, 4096:5120].rearrange("p (a b c) -> p a b c", a=2, b=4)
        Qbd = Rb[:, 5120:5312].rearrange("p (a b) -> p a b", a=4)
        knew = Rb[:, 7232:7744].rearrange("p (a b) -> p a b", a=4)
        Es = R[:, 2656:3040].rearrange("p (a b) -> p a b", a=2)
        sacc = R[:, 3040:3232]
        sden = R[:, 3232:3424]
        sred = R[:, 3424:3616].rearrange("p (a b) -> p a b", a=3)

        def sample_core(i):
            P.op("dve", lambda e: e.memset(Qbd[:, :, :], 0.0), w=["Qbd"])
            P.op("sp", lambda e: e.dma_start(out=knew[:, :, :], in_=kTd.rearrange("(c p) t -> p c t", p=128)[:, :, 4096:4224]),
                 r=["kTd"], w=["knew"], dsem="ktl")
            for b in range(16):
                for hh in range(2):
                    pr = slice(hh * 64, hh * 64 + 64)
                    P.op("dve", lambda e, b=b, hh=hh, pr=pr: e.tensor_copy(
                        out=Qbd[pr, :, hh * 24:(hh + 1) * 24].rearrange("p c (g t) -> p c g t", g=3),
                        in_=QT[pr, :, b * 8:(b + 1) * 8].rearrange("p (g c) t -> p c g t", g=3)), r=["QT", "Qbd"], w=["Qbd"])
                for tl in range(17):
                    sl = tl % 2
                    if tl < 16:
                        P.op("sp", lambda e, b=b, tl=tl, sl=sl: e.dma_start(out=skt[:, sl, :], in_=ck[b, tl * 128:(tl + 1) * 128, :]),
                             w=[KEY("skt", sl)], dsem=f"sk{sl}")
                        P.op("sp", lambda e, b=b, tl=tl, sl=sl: e.dma_start(out=svt[:, sl, :], in_=cv[b, tl * 128:(tl + 1) * 128, :]),
                             w=[KEY("svt", sl)], dsem=f"sv{sl}")
                        for c4 in range(4):
                            P.op("pe", lambda e, c4=c4, sl=sl: e.transpose(ps[1][:, c4 * 128:(c4 + 1) * 128], skt[:, sl, c4 * 128:(c4 + 1) * 128], ident[:, :]),
                                 r=[KEY("skt", sl), "const"], w=[KEY("ps", 1)])
                        P.op("act", lambda e, sl=sl: e.activation(out=skT[:, sl, :, :], in_=ps[1][:, :].rearrange("p (a b) -> p a b", a=4), func=AF.Copy),
                             r=[KEY("ps", 1)], w=[KEY("skT", sl)])
                        for c4 in range(4):
                            P.op("pe", lambda e, c4=c4, sl=sl: e.matmul(ps[2][:, c4 * 48:(c4 + 1) * 48], lhsT=skT[:, sl, c4, :], rhs=Qbd[:, c4, :],
                                                                        start=True, stop=True), r=[KEY("skT", sl), "Qbd"], w=[KEY("ps", 2)])
                        mk = smask[:, tl, :]
                        vsrc = lambda h, sl=sl: svt[:, sl, h * 64:(h + 1) * 64]
                        vkey = KEY("svt", sl)
                    else:
                        for c4 in range(4):
                            P.op("pe", lambda e, c4=c4: e.matmul(ps[2][:, c4 * 48:(c4 + 1) * 48], lhsT=knew[:, c4, :], rhs=Qbd[:, c4, :],
                                                                 start=True, stop=True), r=["knew", "Qbd"], w=[KEY("ps", 2)])
                        mk = smask[:, 17 + b, :]
                        vsrc = lambda h: vnew[:, h * 64:(h + 1) * 64]
                        vkey = "vnew"
                    P.op("act", lambda e, sl=sl: e.activation(out=Es[:, sl, :], in_=ps[2][:, 0:192], func=AF.Exp, scale=0.125),
                         r=[KEY("ps", 2)], w=[KEY("Es", sl)])
                    P.op("dve", lambda e, sl=sl, mk=mk: e.tensor_tensor(
                        out=Es[:, sl, :].rearrange("p (h q) -> p h q", h=8), in0=Es[:, sl, :].rearrange("p (h q) -> p h q", h=8),
                        in1=mk.unsqueeze(1).to_broadcast([128, 8, 24]), op=ALU.mult), r=[KEY("Es", sl), "const"], w=[KEY("Es", sl)])
                    po = 3 + (tl % 2) * 2
                    for h in range(8):
                        P.op("pe", lambda e, h=h, sl=sl, po=po, vsrc=vsrc: e.matmul(ps[po][0:64, h * 24:(h + 1) * 24], lhsT=vsrc(h), rhs=Es[:, sl, h * 24:(h + 1) * 24],
                                                                                    start=True, stop=True), r=[KEY("Es", sl), vkey], w=[KEY("ps", po)])
                    P.op("pe", lambda e, sl=sl, po=po: e.matmul(ps[po][0:64, 192:384], lhsT=onesf[:, :], rhs=Es[:, sl, :], start=True, stop=True),
                         r=[KEY("Es", sl), "const"], w=[KEY("ps", po)])
                    if tl == 0:
                        P.op("dve", lambda e, po=po: e.tensor_copy(out=sacc[0:64, :], in_=ps[po][0:64, 0:192]), r=[KEY("ps", po)], w=["sacc"])
                        P.op("dve", lambda e, po=po: e.tensor_copy(out=sden[0:64, :], in_=ps[po][0:64, 192:384]), r=[KEY("ps", po)], w=["sden"])
                    else:
                        P.op("dve", lambda e, po=po: e.tensor_tensor(out=sacc[0:64, :], in0=ps[po][0:64, 0:192], in1=sacc[0:64, :], op=ALU.add),
                             r=[KEY("ps", po), "sacc"], w=["sacc"])
                        P.op("dve", lambda e, po=po: e.tensor_tensor(out=sden[0:64, :], in0=ps[po][0:64, 192:384], in1=sden[0:64, :], op=ALU.add),
                             r=[KEY("ps", po), "sden"], w=["sden"])
                a4 = sacc[0:64, :].rearrange("p (h g t) -> p h g t", h=8, g=3)
                d4 = sden[0:64, :].rearrange("p (h g t) -> p h g t", h=8, g=3)
                n_ = sred[0:64, 0, :].rearrange("p (h t) -> p h t", h=8)
                d_ = sred[0:64, 1, :].rearrange("p (h t) -> p h t", h=8)
                P.op("dve", lambda e, a4=a4, n_=n_: e.tensor_tensor(out=n_, in0=a4[:, :, 0, :], in1=a4[:, :, 1, :], op=ALU.add), r=["sacc"], w=["sred0"])
                P.op("dve", lambda e, a4=a4, n_=n_: e.tensor_tensor(out=n_, in0=n_, in1=a4[:, :, 2, :], op=ALU.add), r=["sacc", "sred0"], w=["sred0"])
                P.op("dve", lambda e, d4=d4, d_=d_: e.tensor_tensor(out=d_, in0=d4[:, :, 0, :], in1=d4[:, :, 1, :], op=ALU.add), r=["sden"], w=["sred1"])
                P.op("dve", lambda e, d4=d4, d_=d_: e.tensor_tensor(out=d_, in0=d_, in1=d4[:, :, 2, :], op=ALU.add), r=["sden", "sred1"], w=["sred1"])
                P.op("dve", lambda e, d_=d_: e.reciprocal(out=d_, in_=d_), r=["sred1"], w=["sred1"])
                P.op("dve", lambda e, b=b, n_=n_, d_=d_: e.tensor_tensor(out=combT[:, :, b * 8:(b + 1) * 8], in0=n_, in1=d_, op=ALU.mult),
                     r=["sred0", "sred1"], w=["comb"])

        blkB = [(0, 128)] + [(256 + 512 * k, 512) for k in range(4)]
        for bi, (c0, N) in enumerate(blkB):
            xk = KEY("x", bi)
            for i in range(2):
                if STAGE < 4:
                    break
                l = 2 + i
                ffn(l * 2 + 0, c0, N, xk)
                attention(i, bi, c0, N, xk)
                ffn(l * 2 + 1, c0, N, xk)
            for tt in range(N // 128):
                orow = tt * 128 if bi == 0 else 128 + (c0 - 256) + tt * 128

                def dma_y(orow=orow):
                    P.op("sp", lambda e: e.dma_start(out=y_o[orow:orow + 128, :], in_=tokst[:, :]), r=["tokst"], w=[KEY("yo", orow)], dsem="tk")
                transpose_out(lambda dc, tt=tt, c0=c0: xT[:, dc, c0 + tt * 128:c0 + (tt + 1) * 128], 128, dma_y, [xk], "y")

        P.op("sp", lambda e: None, r=list(P.last_w.keys()), w=[])
        P.emit(nc, st)
    P.selfcheck()
    return nc


def blk_of_col(col):
    return 0 if col < 256 else 1 + (col - 256) // 512


_NC_CACHE = {}


def _pcolize(v):
    v = np.asarray(v, np.float32)
    return np.ascontiguousarray(v.reshape(-1, 128).T)


def _prep_shared(inp):
    f = lambda a: np.asarray(a, np.float32)
    sh = {}
    G, U, Dn = f(inp["ffn_w_gate"]), f(inp["ffn_w_up"]), f(inp["ffn_w_down"])
    def kmaj(W):
        return W.reshape(8, 128, W.shape[1]).transpose(1, 0, 2)
    def pieces(W3, w):
        F_ = W3.shape[2]
        return np.stack([W3[:, :, i * w:(i + 1) * w].reshape(128, -1) for i in range(F_ // w)])
    sh["wg"] = np.ascontiguousarray(np.stack([pieces(kmaj(G[l, s]), 256) for l in range(4) for s in range(2)]))
    sh["wu"] = np.ascontiguousarray(np.stack([pieces(kmaj(U[l, s]), 256) for l in range(4) for s in range(2)]))
    def dpieces(W):
        W3 = W.reshape(22, 128, 1024).transpose(1, 0, 2)
        return np.stack([W3[:, :, i * 128:(i + 1) * 128].reshape(128, -1) for i in range(8)])
    sh["wd"] = np.ascontiguousarray(np.stack([dpieces(Dn[l, s]) for l in range(4) for s in range(2)]))
    W1 = f(inp["conv_w1"]); W2 = f(inp["conv_w2"])
    w1 = []
    for l in range(2):
        W3 = kmaj(W1[l])
        w1.append(np.stack([np.concatenate([W3[:, :, dc * 128:(dc + 1) * 128], W3[:, :, 1024 + dc * 128:1024 + (dc + 1) * 128]], axis=2).reshape(128, -1)
                            for dc in range(8)]))
    sh["w1"] = np.ascontiguousarray(np.stack(w1))
    sh["w2"] = np.ascontiguousarray(np.stack([pieces(kmaj(W2[l]), 256) for l in range(2)]))
    sh["wk"] = np.ascontiguousarray(pieces(kmaj(f(inp["w_k"])), 256))
    sh["wv"] = np.ascontiguousarray(pieces(kmaj(f(inp["w_v"])), 256))
    sh["wq"] = np.ascontiguousarray(np.stack([pieces(kmaj(f(inp["w_q"])[i]), 256) for i in range(2)]))
    wo = []
    for i in range(2):
        W3 = f(inp["w_o"])[i].reshape(8, 64, 1024).transpose(1, 0, 2)
        wo.append(np.stack([W3[:, :, pc * 256:(pc + 1) * 256].reshape(64, -1) for pc in range(4)]))
    sh["wo"] = np.ascontiguousarray(np.stack(wo))
    pc = np.zeros((128, NCOL), np.float32)
    fn = f(inp["ffn_norm"])
    for l in range(4):
        for s in range(2):
            pc[:, C_FFN + (l * 2 + s) * 8:C_FFN + (l * 2 + s) * 8 + 8] = _pcolize(fn[l, s])
    for l in range(2):
        pc[:, C_CN + l * 8:C_CN + l * 8 + 8] = _pcolize(f(inp["conv_norm"])[l])
        pc[:, C_B1 + l * 16:C_B1 + l * 16 + 16] = _pcolize(f(inp["conv_b1"])[l])
        dw = f(inp["conv_dw"])[l]
        for k in range(31):
            pc[:, C_DW + l * 248 + k * 8:C_DW + l * 248 + k * 8 + 8] = _pcolize(dw[k])
        pc[:, C_DWB + l * 8:C_DWB + l * 8 + 8] = _pcolize(f(inp["conv_dw_b"])[l])
        pc[:, C_LNG + l * 8:C_LNG + l * 8 + 8] = _pcolize(f(inp["conv_ln_g"])[l])
        pc[:, C_LNB + l * 8:C_LNB + l * 8 + 8] = _pcolize(f(inp["conv_ln_b"])[l])
        pc[:, C_B2 + l * 8:C_B2 + l * 8 + 8] = _pcolize(f(inp["conv_b2"])[l])
        pc[:, C_AN + l * 8:C_AN + l * 8 + 8] = _pcolize(f(inp["attn_norm"])[l])
    pc[:, C_KVN:C_KVN + 8] = _pcolize(f(inp["kv_norm"]))
    sh["pcol"] = pc
    pr = np.concatenate([f(inp["k_norm"]), f(inp["q_norm"])[0], f(inp["q_norm"])[1]])[None, :]
    sh["prow"] = np.ascontiguousarray(np.broadcast_to(pr, (128, 192))).astype(np.float32)
    sh["ident"] = np.eye(128, dtype=np.float32)
    sm = np.zeros((128, 33, 24), np.float32)
    dils = (1, 4, 16)
    p = np.arange(128)
    for tl in range(16):
        row = tl * 128 + p
        for g, dil in enumerate(dils):
            for t in range(8):
                delta = 2048 + t - row
                sm[:, tl, g * 8 + t] = ((delta > 0) & (delta % dil == 0) & (delta // dil <= 128)).astype(np.float32)
    bq, tq = p // 8, p % 8
    for b in range(16):
        for g, dil in enumerate(dils):
            for t in range(8):
                dlt = t - tq
                sm[:, 17 + b, g * 8 + t] = ((bq == b) & (dlt >= 0) & (dlt % dil == 0)).astype(np.float32)
    sh["smask"] = sm.reshape(128, 33 * 24)
    return sh


def _rope_tab(j):
    inv = (500000.0 ** (-(np.arange(0, 16, 2, dtype=np.float32) / np.float32(16)))).astype(np.float32)
    tab = np.zeros((128, 18, 16), np.float32)
    p = np.arange(128)
    for gt in range(18):
        if gt == 0:
            pos = 2048 + (p % 8)
        elif gt == 1:
            pos = np.zeros(128, np.int64)
        else:
            pos = 2048 * j + 128 * (gt - 2) + p
        ang = pos.astype(np.float32)[:, None] * inv[None, :]
        tab[:, gt, 0:8] = np.cos(ang).astype(np.float32)
        tab[:, gt, 8:16] = np.sin(ang).astype(np.float32)
    return tab.reshape(128, 18 * 16)


def _masks(has_prev):
    m = np.zeros((128, 768), np.float32)
    p = np.arange(128)[:, None]
    q = np.arange(128)[None, :]
    prev = (p >= q).astype(np.float32)
    cur = (p <= q).astype(np.float32)
    m[:, 0:128] = prev; m[:, 128:256] = cur
    m[:, 256:384] = prev * has_prev; m[:, 384:512] = cur
    for k in range(4):
        iq = 32 * k + np.arange(32)[None, :]
        m[:, 512 + 64 * k:512 + 64 * k + 32] = (p >= iq).astype(np.float32) * has_prev
        m[:, 512 + 64 * k + 32:512 + 64 * k + 64] = (p <= iq).astype(np.float32)
    return m.astype(ml_dtypes.bfloat16)


def kernel(**inp):
    if "nc" not in _NC_CACHE:
        _NC_CACHE["nc"] = build_program()
    nc = _NC_CACHE["nc"]
    sh = _prep_shared(inp)
    xp = np.asarray(inp["x_prompt"], np.float32)
    xs = np.asarray(inp["x_sample"], np.float32)
    sc = np.asarray(inp["state_conv"], np.float32)
    ck = np.asarray(inp["cache_k"], np.float32)
    cv = np.asarray(inp["cache_v"], np.float32)
    in_maps = []
    for c in range(8):
        b, j = c // 4, c % 4
        has_prev = 1.0 if j > 0 else 0.0
        halo = xp[b, 2048 * j - 128:2048 * j] if j > 0 else np.zeros((128, D), np.float32)
        xin = np.concatenate([xs[16 * c:16 * c + 16].reshape(128, D), halo, xp[b, 2048 * j:2048 * (j + 1)]], axis=0)
        pc = sh["pcol"].copy()
        pc[:, C_FLAG] = has_prev
        if j > 0:
            pc[:, C_SEL + (c - 1)] = 1.0
        m = dict(sh)
        m["pcol"] = pc
        m["xin"] = np.ascontiguousarray(xin)
        m["sconv"] = np.ascontiguousarray(sc[:, 16 * c:16 * c + 16].reshape(2, 480, D))
        m["ck"] = np.ascontiguousarray(ck[16 * c:16 * c + 16].reshape(16, 2048, 512))
        m["cv"] = np.ascontiguousarray(cv[16 * c:16 * c + 16].reshape(16, 2048, 512))
        m["rope"] = _rope_tab(j)
        m["masks"] = _masks(has_prev)
        in_maps.append(m)
    res = run_bass_kernel_spmd(nc, in_maps, core_ids=list(range(8)))
    R_ = res.results
    y_prompt = np.stack([np.concatenate([R_[b * 4 + j]["y"][128:] for j in range(4)], axis=0) for b in range(2)])
    y_sample = np.concatenate([R_[c]["y"][0:128].reshape(16, 8, D) for c in range(8)], axis=0)
    conv_p = np.stack([R_[3]["scp"], R_[7]["scp"]], axis=1)
    conv_s = np.concatenate([R_[c]["scs"] for c in range(8)], axis=1)
    k_p = np.stack([R_[3]["kp"], R_[7]["kp"]]).reshape(2, 2048, 8, 64)
    v_p = np.stack([R_[3]["vp"], R_[7]["vp"]]).reshape(2, 2048, 8, 64)
    k_s = np.concatenate([R_[c]["ks"] for c in range(8)], axis=0).reshape(128, 2048, 8, 64)
    v_s = np.concatenate([R_[c]["vs"] for c in range(8)], axis=0).reshape(128, 2048, 8, 64)
    f32 = lambda a: np.ascontiguousarray(a, dtype=np.float32)
    return (f32(y_prompt), f32(y_sample), f32(conv_p), f32(conv_s), f32(k_p), f32(v_p), f32(k_s), f32(v_s))
```
